# Optimizing a Trainium2 kernel written in Bass

```python
import jax, jax.numpy as jnp
from jax import lax
import numpy as np

D_MODEL = 2048
BATCH = 1
SEQ = 8192
DEPTH = 2

GRID_W = 64
CTX_LEN = 256
N_EVEN = (DEPTH + 1) // 2
N_ODD = DEPTH // 2

HG_HEADS = 8
HG_DK = 128
HG_DV = 128
HG_WIDTH = HG_HEADS * HG_DK
SG_GROUPS = 8
SG_CH = 128
SG_WIDTH = SG_GROUPS * SG_CH
SG_CHUNK = 128
FT_GROUPS = 4
IN_SIZES = (HG_WIDTH, HG_WIDTH, HG_WIDTH, HG_WIDTH, HG_WIDTH, SG_WIDTH, SG_WIDTH)
IN_WIDTH = sum(IN_SIZES)
IN_SPLITS = tuple(int(s) for s in np.cumsum(IN_SIZES)[:-1])
HG_STATE_COLS = 3 * HG_WIDTH
MIX_WIDTH = HG_WIDTH + SG_WIDTH
D_FF = 4 * D_MODEL
N_MOD = 6
EPS = 1e-6

kernel_name = "hybrid_hgrn2_gmlp_fnet_flow_block"


def rmsnorm(x, g):
    x32 = x.astype(jnp.float32)
    y = x32 * lax.rsqrt(jnp.mean(jnp.square(x32), axis=-1, keepdims=True) + EPS)
    return (y * g).astype(x.dtype)


def layernorm(x, g, b):
    x32 = x.astype(jnp.float32)
    mu = jnp.mean(x32, axis=-1, keepdims=True)
    xc = x32 - mu
    y = xc * lax.rsqrt(jnp.mean(jnp.square(xc), axis=-1, keepdims=True) + EPS)
    return (y * g + b).astype(x.dtype)


def modulate(x, gain, shift, scale):
    return rmsnorm(x, gain) * (1 + scale) + shift


def to_heads(t):
    return t.astype(jnp.float32).reshape(t.shape[0], t.shape[1], HG_HEADS, -1)


def flip(t):
    return jnp.flip(t, axis=1)


def hgrn_gates(f_raw, lb):
    f = lb + (1.0 - lb) * jax.nn.sigmoid(f_raw)
    return jnp.log(f), 1.0 - f


def gla_scan(q, k, v, logf, s0):
    b_, L, H, _ = q.shape
    dv = v.shape[-1]
    rows = L // GRID_W

    def blocks(t):
        return t.reshape(b_, rows, GRID_W, H, t.shape[-1]).transpose(1, 0, 3, 2, 4)

    mask = jnp.tril(jnp.ones((GRID_W, GRID_W), dtype=bool))[:, :, None]

    def step(S, blk):
        qc, kc, vc, gc = blk
        bcum = jnp.cumsum(gc, axis=2)
        diff = bcum[:, :, :, None, :] - bcum[:, :, None, :, :]
        decay = jnp.exp(jnp.where(mask, diff, -jnp.inf))
        scores = jnp.einsum('bhtsk,bhsk->bhts', qc[:, :, :, None, :] * decay, kc)
        o = (jnp.einsum('bhts,bhsv->bhtv', scores, vc)
             + jnp.einsum('bhtk,bhkv->bhtv', qc * jnp.exp(bcum), S))
        blast = bcum[:, :, -1:, :]
        S = (jnp.exp(blast[:, :, 0, :])[..., None] * S
             + jnp.einsum('bhsk,bhsv->bhkv', kc * jnp.exp(blast - bcum), vc))
        return S, o

    S, o = lax.scan(step, s0, (blocks(q), blocks(k), blocks(v), blocks(logf)))
    o = o.transpose(1, 0, 3, 2, 4).reshape(b_, L, H, dv)
    return o, S


def gla_final_state(k, v, logf):
    bcum = jnp.cumsum(logf, axis=1)
    w = jnp.exp(bcum[:, -1:] - bcum)
    return jnp.einsum('blhk,blhv->bhkv', k * w, v)


def context_states(hc, lb_l, w_in_e):
    z = hc @ w_in_e[:, :HG_STATE_COLS]
    f_fr, f_br, i_r = jnp.split(z, 3, axis=-1)
    v = to_heads(i_r)
    lf_f, k_f = hgrn_gates(to_heads(f_fr), lb_l[0].reshape(HG_HEADS, HG_DK))
    lf_b, k_b = hgrn_gates(to_heads(f_br), lb_l[1].reshape(HG_HEADS, HG_DK))
    sf = gla_final_state(k_f, v, lf_f)
    sb = gla_final_state(flip(k_b), flip(v), flip(lf_b))
    return sf, sb


def spatial_gate(u_r, v_r, w_s, b_s, ln_g, ln_b):
    B, L, _ = u_r.shape
    u = jax.nn.gelu(u_r)
    v = layernorm(jax.nn.gelu(v_r), ln_g, ln_b)
    v = v.reshape(B, L // SG_CHUNK, SG_CHUNK, SG_GROUPS, SG_CH)
    mixed = jnp.einsum('gts,bnsgc->bntgc', w_s, v) + b_s.T[:, :, None]
    return u * mixed.reshape(B, L, SG_WIDTH)


def even_mixer(h, lb_l, w_in_e, w_out_e, hg_gain_e, sg_w_e, sg_b_e, sg_ln_g_e, sg_ln_b_e, s0f, s0b):
    B, L, _ = h.shape
    z = h @ w_in_e
    f_fr, f_br, i_r, q_r, g_r, u_r, v_r = jnp.split(z, IN_SPLITS, axis=-1)
    lf_f, k_f = hgrn_gates(to_heads(f_fr), lb_l[0].reshape(HG_HEADS, HG_DK))
    lf_b, k_b = hgrn_gates(to_heads(f_br), lb_l[1].reshape(HG_HEADS, HG_DK))
    q = to_heads(jax.nn.silu(q_r)) * (HG_DK ** -0.5)
    i = to_heads(i_r)
    o_f, sf = gla_scan(q, k_f, i, lf_f, s0f)
    o_b, sb = gla_scan(flip(q), flip(k_b), flip(i), flip(lf_b), s0b)
    o = rmsnorm(o_f + flip(o_b), hg_gain_e.reshape(HG_HEADS, HG_DV))
    o = o.reshape(B, L, HG_WIDTH).astype(h.dtype) * jax.nn.silu(g_r)
    s = spatial_gate(u_r, v_r, sg_w_e, sg_b_e, sg_ln_g_e, sg_ln_b_e)
    out = jnp.concatenate([o, s], axis=-1) @ w_out_e
    return out, sf, sb


def fourier_mixer(h, w_o):
    B, L, D = h.shape
    hg = h.astype(jnp.float32).reshape(B, L, FT_GROUPS, D // FT_GROUPS)
    y = jnp.fft.fft2(hg, axes=(1, 3), norm="ortho").real
    return y.reshape(B, L, D).astype(h.dtype) @ w_o


def sq_relu_mlp(h, w1, w2):
    return jnp.square(jax.nn.relu(h @ w1)) @ w2


def setup_inputs(seed: int = 0) -> dict:
    key = jax.random.key(seed)
    ks = jax.random.split(key, 20)
    f32 = jnp.float32
    nrm = lambda k, shape, s: jax.random.normal(k, shape, f32) * s
    return {
        "x": nrm(ks[0], (BATCH, SEQ, D_MODEL), 1.0),
        "c": nrm(ks[1], (BATCH, D_MODEL), 1.0),
        "ctx": nrm(ks[2], (BATCH, CTX_LEN, D_MODEL), 1.0),
        "c_ctx": nrm(ks[3], (D_MODEL,), 1.0),
        "w_ada": nrm(ks[4], (DEPTH, D_MODEL, N_MOD * D_MODEL), 0.5 * D_MODEL ** -0.5),
        "b_ada": nrm(ks[5], (DEPTH, N_MOD * D_MODEL), 0.02),
        "norm_gain": 1.0 + nrm(ks[6], (DEPTH, 4, D_MODEL), 0.02),
        "w_in": nrm(ks[7], (N_EVEN, D_MODEL, IN_WIDTH), D_MODEL ** -0.5),
        "w_out": nrm(ks[8], (N_EVEN, MIX_WIDTH, D_MODEL), MIX_WIDTH ** -0.5),
        "lb_raw": nrm(ks[9], (2, DEPTH + 1, HG_WIDTH), 0.1),
        "hg_norm_gain": 1.0 + nrm(ks[10], (N_EVEN, HG_WIDTH), 0.02),
        "sg_w": nrm(ks[11], (N_EVEN, SG_GROUPS, SG_CHUNK, SG_CHUNK), SG_CHUNK ** -0.5),
        "sg_b": 1.0 + nrm(ks[12], (N_EVEN, SG_GROUPS, SG_CHUNK), 0.02),
        "sg_ln_gain": 1.0 + nrm(ks[13], (N_EVEN, SG_WIDTH), 0.02),
        "sg_ln_bias": nrm(ks[14], (N_EVEN, SG_WIDTH), 0.02),
        "w_fourier": nrm(ks[15], (N_ODD, D_MODEL, D_MODEL), D_MODEL ** -0.5),
        "w_mlp_in": nrm(ks[16], (DEPTH, D_MODEL, D_FF), D_MODEL ** -0.5),
        "w_mlp_out": nrm(ks[17], (DEPTH, D_FF, D_MODEL), D_FF ** -0.5),
    }


def reference(x, c, ctx, c_ctx, w_ada, b_ada, norm_gain, w_in, w_out, lb_raw, hg_norm_gain,
              sg_w, sg_b, sg_ln_gain, sg_ln_bias, w_fourier, w_mlp_in, w_mlp_out):
    lb_all = jnp.cumsum(jax.nn.softmax(lb_raw.astype(jnp.float32), axis=1), axis=1)
    h_ctx = ctx
    for l in range(DEPTH):
        g_pre_m, g_post_m, g_pre_f, g_post_f = norm_gain[l]
        mx = [m[:, None, :] for m in jnp.split(jax.nn.silu(c) @ w_ada[l] + b_ada[l], N_MOD, axis=-1)]
        ctx_later = any(j % 2 == 0 for j in range(l + 1, DEPTH))
        need_ctx_here = (l % 2 == 0) or ctx_later
        if need_ctx_here:
            mc = jnp.split(jax.nn.silu(c_ctx) @ w_ada[l] + b_ada[l], N_MOD, axis=-1)
            hc = modulate(h_ctx, g_pre_m, mc[0], mc[1])
        hx = modulate(x, g_pre_m, mx[0], mx[1])
        if l % 2 == 0:
            e = l // 2
            lb_l = lb_all[:, l]
            params = (w_in[e], w_out[e], hg_norm_gain[e], sg_w[e], sg_b[e], sg_ln_gain[e], sg_ln_bias[e])
            if ctx_later:
                zero = jnp.zeros((h_ctx.shape[0], HG_HEADS, HG_DK, HG_DV), jnp.float32)
                mix_c, sf, sb = even_mixer(hc, lb_l, *params, zero, zero)
            else:
                sf, sb = context_states(hc, lb_l, w_in[e])
            mix_x, _, _ = even_mixer(hx, lb_l, *params, sf, sb)
        else:
            o = l // 2
            mix_x = fourier_mixer(hx, w_fourier[o])
            if ctx_later:
                mix_c = fourier_mixer(hc, w_fourier[o])
        x = x + mx[2] * rmsnorm(mix_x, g_post_m)
        x = x + mx[5] * rmsnorm(sq_relu_mlp(modulate(x, g_pre_f, mx[3], mx[4]), w_mlp_in[l], w_mlp_out[l]), g_post_f)
        if ctx_later:
            h_ctx = h_ctx + mc[2] * rmsnorm(mix_c, g_post_m)
            h_ctx = h_ctx + mc[5] * rmsnorm(
                sq_relu_mlp(modulate(h_ctx, g_pre_f, mc[3], mc[4]), w_mlp_in[l], w_mlp_out[l]), g_post_f)
    return x
```

```python
import math
from contextlib import ExitStack

import numpy as np
import ml_dtypes
import concourse.bass as bass
import concourse.mybir as mybir
from concourse.bass_utils import run_bass_kernel_spmd

F32 = mybir.dt.float32
BF16 = mybir.dt.bfloat16
AF = mybir.ActivationFunctionType
ALU = mybir.AluOpType
ENGS = ["pe", "act", "dve", "pool", "sp"]
NCORES = 8
D = 2048
SEQ = 8192
EPS = 1e-6


class Prog:
    def __init__(self, nc, stack):
        self.nc = nc
        self.stack = stack
        self.q = {e: [] for e in ENGS}
        self.semh = {}
        self.cnt = {}
        for e in ENGS:
            self.semh[e] = stack.enter_context(nc.semaphore("s_" + e))
            self.cnt[e] = 0
        self.waited = {e: {} for e in ENGS}
        self.lastw = {}
        self.readers = {}
        self.dma_keys = []
        self.al = {}

    def alias(self, a, b):
        self.al.setdefault(a, set()).add(b)
        self.al.setdefault(b, set()).add(a)

    def _exp(self, keys):
        out = []
        for k in keys:
            out.append(k)
            for a in self.al.get(k, ()):
                out.append(a)
        return out

    def _sem(self, key):
        if key not in self.semh:
            self.semh[key] = self.stack.enter_context(self.nc.semaphore("d_" + str(key)))
            self.cnt[key] = 0
            self.dma_keys.append(key)
        return self.semh[key]

    def _deps(self, eng, reads, writes, is_dma=False):
        deps = set()
        for b in self._exp(reads):
            if b in self.lastw:
                deps.add(self.lastw[b])
        for b in self._exp(writes):
            if b in self.lastw:
                deps.add(self.lastw[b])
            for r in self.readers.get(b, ()):
                deps.add(r)
        waits = {}
        for (s, v) in deps:
            if s == eng and not is_dma and eng in ("pe", "sp"):
                continue
            if self.waited[eng].get(s, 0) >= v:
                continue
            if waits.get(s, 0) < v:
                waits[s] = v
        for s, v in waits.items():
            self.waited[eng][s] = v
        return list(waits.items())

    def _commit(self, ev, reads, writes):
        for b in writes:
            self.lastw[b] = ev
            self.readers[b] = []
        for b in reads:
            if b in writes:
                continue
            self.readers.setdefault(b, []).append(ev)

    def op(self, eng, fn, reads=(), writes=()):
        waits = self._deps(eng, reads, writes)
        self.cnt[eng] += 1
        ev = (eng, self.cnt[eng])
        self.q[eng].append((waits, fn, (eng, 1)))
        self._commit(ev, reads, writes)
        return ev

    def dma(self, eng, key, fn, reads=(), writes=(), n=1):
        self._sem(key)
        waits = self._deps(eng, reads, writes, is_dma=True)
        self.cnt[key] += 16 * n
        ev = (key, self.cnt[key])
        self.q[eng].append((waits, fn, (key, 16)))
        self._commit(ev, reads, writes)
        return ev

    def finish(self, eng="sp"):
        waits = []
        for k in self.dma_keys:
            if self.waited[eng].get(k, 0) < self.cnt[k]:
                waits.append((k, self.cnt[k]))
        for e in ENGS:
            if e != eng and self.cnt[e] > 0 and self.waited[eng].get(e, 0) < self.cnt[e]:
                waits.append((e, self.cnt[e]))
        self.q[eng].append((waits, None, None))

    def emit(self):
        semh = self.semh
        q = self.q

        def replay(name, e):
            for waits, fn, inc in q[name]:
                for s, v in waits:
                    e.wait_ge(semh[s], v)
                if fn is None:
                    continue
                r = fn(e)
                if isinstance(r, (list, tuple)):
                    for ins in r:
                        ins.then_inc(semh[inc[0]], inc[1])
                else:
                    r.then_inc(semh[inc[0]], inc[1])

        with self.nc.Block() as block:
            @block.tensor
            def _(e):
                replay("pe", e)

            @block.scalar
            def _(e):
                replay("act", e)

            @block.vector
            def _(e):
                replay("dve", e)

            @block.gpsimd
            def _(e):
                replay("pool", e)

            @block.sync
            def _(e):
                replay("sp", e)


class Ctx:
    def __init__(self):
        self.nc = bass.Bass("TRN2", target_bir_lowering=False)
        self.st = ExitStack()
        self.P = Prog(self.nc, self.st)

    def din(self, name, shape, dt=F32):
        return self.nc.dram_tensor(name, list(shape), dt, kind="ExternalInput").ap()

    def dout(self, name, shape, dt=F32):
        return self.nc.dram_tensor(name, list(shape), dt, kind="ExternalOutput").ap()

    def dint(self, name, shape, dt=F32):
        return self.nc.dram_tensor(name, list(shape), dt, kind="Internal").ap()

    def sb(self, name, shape, dt=F32):
        return self.st.enter_context(self.nc.sbuf_tensor(name, list(shape), dt))

    def ps(self, name, shape, dt=F32):
        return self.st.enter_context(self.nc.psum_tensor(name, list(shape), dt))

    def done(self):
        self.P.finish()
        self.P.emit()
        self.st.close()
        return self.nc


def _bf(a):
    return np.ascontiguousarray(a).astype(ml_dtypes.bfloat16)


def build_mods():
    C = Ctx()
    P = C.P
    NW = 1536
    cT = C.din("cT", [128, 32])
    wa = C.din("wa", [2, D, NW])
    ba = C.din("ba", [2, NW])
    out = C.dout("mods", [2, 2, NW])
    cs = C.sb("cs", [128, 32])
    ss = C.sb("ss", [128, 32])
    wst = [C.sb(f"wst{i}", [128, 16, 512]) for i in range(2)]
    bias = C.sb("bias", [2, 2, NW])
    res = C.sb("res", [2, 2, NW])
    pz = [C.ps(f"pz{i}", [128, 512]) for i in range(2)]
    P.dma("sp", "ldc", lambda e: e.dma_start(out=cs[:], in_=cT[:, :]), writes=["cs"])
    P.dma("sp", "ldb", lambda e: [e.dma_start(out=bias[0:1, :, :], in_=ba[None, :, :]),
                                  e.dma_start(out=bias[1:2, :, :], in_=ba[None, :, :])], writes=["bias"], n=2)
    P.op("act", lambda e: e.activation(out=ss[:], in_=cs[:], func=AF.Silu), reads=["cs"], writes=["ss"])
    it = 0
    for l in range(2):
        wv = wa[l].rearrange("(kc p) n -> p kc n", p=128)
        for qn in range(3):
            b = it % 2
            P.dma("sp", f"ldw{b}", lambda e, b=b, wv=wv, qn=qn: e.dma_start(out=wst[b][:], in_=wv[:, :, qn * 512:(qn + 1) * 512]),
                  writes=[f"wst{b}"])

            def mm(e, b=b):
                r = None
                for kc in range(16):
                    r = e.matmul(pz[b][0:2, :], lhsT=ss[:, kc * 2:kc * 2 + 2], rhs=wst[b][:, kc, :], start=(kc == 0), stop=(kc == 15))
                return r
            P.op("pe", mm, reads=["ss", f"wst{b}"], writes=[f"pz{b}"])
            P.op("dve", lambda e, b=b, l=l, qn=qn: e.tensor_tensor(out=res[:, l, qn * 512:(qn + 1) * 512], in0=pz[b][0:2, :],
                                                                  in1=bias[:, l, qn * 512:(qn + 1) * 512], op=ALU.add),
                 reads=[f"pz{b}", "bias"], writes=["res"])
            it += 1
    P.dma("sp", "st", lambda e: e.dma_start(out=out[:, :, :], in_=res[:]), reads=["res"])
    return C.done()


def run_mods(c, c_ctx, w_ada, b_ada):
    nc = build_mods()
    cT = np.zeros((128, 32), np.float32)
    cv = np.stack([c.reshape(-1), c_ctx.reshape(-1)], 0)
    cT[:] = cv.reshape(2, 16, 128).transpose(2, 1, 0).reshape(128, 32)
    ims = []
    for j in range(NCORES):
        sl = slice(j * 1536, (j + 1) * 1536)
        ims.append({"cT": cT, "wa": np.ascontiguousarray(w_ada[:, :, sl]), "ba": np.ascontiguousarray(b_ada[:, sl])})
    res = run_bass_kernel_spmd(nc, ims, core_ids=list(range(NCORES)))
    mods = np.concatenate([res.results[j]["mods"] for j in range(NCORES)], axis=2)
    return mods


def hgrn_consts():
    s = np.arange(128)[:, None]
    t = np.arange(128)[None, :]
    cst = np.zeros((128, 900), np.float32)
    cst[:, 0:128] = (s <= t).astype(np.float32) - (s <= 63).astype(np.float32)
    cst[:, 128] = (s[:, 0] <= 63)
    cst[:, 129] = 1.0
    cst[:, 130:258] = (s >= t).astype(np.float32) - (s >= 64).astype(np.float32)
    cst[:, 258] = (s[:, 0] >= 64)
    cst[:, 259] = 1.0
    cst[:, 260:388] = (s > t)
    cst[:, 388:516] = (s < t)
    cst[:, 516:644] = (s <= t)
    cst[:, 644:772] = (s >= t)
    cst[:, 772:900] = np.eye(128)
    return cst


def emit_norm_stats(P, e_in, key_in, junk, ss, rs, epsb, n_feat, tag):
    P.op("pool", lambda e: e.memset(ss[:], 0.0), writes=[tag + "ss"])
    P.op("act", lambda e: e.activation(out=junk, in_=e_in, func=AF.Square, accum_out=ss[:]),
         reads=[key_in], writes=[tag + "junk", tag + "ss"])
    P.op("act", lambda e: e.activation(out=rs[:], in_=ss[:], func=AF.Sqrt, scale=1.0 / n_feat, bias=epsb[:]),
         reads=[tag + "ss"], writes=[tag + "rs"])
    P.op("dve", lambda e: e.reciprocal(out=rs[:], in_=rs[:]), reads=[tag + "rs"], writes=[tag + "rs"])


def build_hgrn():
    C = Ctx()
    P = C.P
    NT = 66
    xin = C.din("xin", [NT * 128, D])
    win = C.din("win", [D, 512])
    rows = C.din("rows", [5, D])
    lbr = C.din("lbr", [1, 768])
    hgg = C.din("hgg", [1, 128])
    cstd = C.din("cst", [128, 900])
    outd = C.dout("o_n", [SEQ, 128])

    cst = C.sb("cst_sb", [128, 900])
    identb = C.sb("identb", [128, 128], BF16)
    epsb = C.sb("epsb", [128, 1])
    Gt = [C.sb(f"G{k}", [128, D]) for k in range(2)]
    Sh = [C.sb(f"Sh{k}", [128, D]) for k in range(2)]
    big = C.sb("big", [128, 8192])
    wb = C.sb("wb", [128, 16, 512], BF16)
    lbt = C.sb("lbt", [128, 768])
    LB = C.sb("LB", [128, 256])
    OML = C.sb("OML", [128, 256])
    HG = C.sb("HG", [128, 128])
    xt = [C.sb(f"xt{i}", [128, D]) for i in range(2)]
    hx = C.sb("hx", [128, D], BF16)
    hxT = C.sb("hxT", [128, D], BF16)
    junk = C.sb("junk", [128, D], BF16)
    ss = C.sb("ss", [128, 1]); rs = C.sb("rs", [128, 1])
    ss2 = C.sb("ss2", [128, 1]); rs2 = C.sb("rs2", [128, 1])
    sig = C.sb("sig", [128, 256]); logf = C.sb("logf", [128, 256]); kk = C.sb("kk", [128, 256])
    qs = C.sb("qs", [128, 128])
    Ep = C.sb("Ep", [128, 256]); En = C.sb("En", [128, 256]); em = C.sb("em", [128, 4]); Erev = C.sb("Erev", [128, 256])
    qtf = C.sb("qtf", [128, 128], BF16); ktf = C.sb("ktf", [128, 128], BF16); ktb = C.sb("ktb", [128, 128], BF16)
    kdf = C.sb("kdf", [128, 128], BF16)
    PTf = C.sb("PTf", [128, 128], BF16); PTb = C.sb("PTb", [128, 128], BF16)
    Sf = C.sb("Sf", [128, 128]); Sb_ = C.sb("Sb", [128, 128])
    Spf = C.sb("Spf", [128, 128], BF16); Spb = C.sb("Spb", [128, 128], BF16)
    qtb_all = C.sb("qtb_all", [128, NT, 128], BF16)
    kdb_all = C.sb("kdb_all", [128, NT, 128], BF16)
    vb_all = C.sb("vb_all", [128, NT, 128], BF16)
    emb_all = C.sb("emb_all", [128, NT, 2])
    ofin = C.sb("ofin", [128, 128]); junk2 = C.sb("junk2", [128, 128])
    obuf = [C.sb(f"obuf{i}", [128, 128]) for i in range(2)]
    wstg = big[:, 0:8192].rearrange("p (k n) -> p k n", k=16)
    oacc = big[:, 0:8192].rearrange("p (t n) -> p t n", t=64)

    pTa = C.ps("pTa", [128, 1024], BF16); pTb = C.ps("pTb", [128, 1024], BF16)
    zps = C.ps("zps", [128, 512]); bcT = C.ps("bcT", [128, 512]); rev = C.ps("rev", [128, 512])
    tq = C.ps("tq", [128, 512]); sc = C.ps("sc", [128, 512]); ops = C.ps("ops", [128, 512])

    P.dma("sp", "ldc", lambda e: [e.dma_start(out=cst[:], in_=cstd[:, :]),
                                  e.dma_start(out=lbt[:], in_=lbr[0:1, :].to_broadcast([128, 768])),
                                  e.dma_start(out=HG[:], in_=hgg[0:1, :].to_broadcast([128, 128]))],
          writes=["cst", "lbt", "HG"], n=3)
    P.op("pool", lambda e: e.memset(epsb[:], EPS), writes=["epsb"])
    P.op("dve", lambda e: e.tensor_copy(out=identb[:], in_=cst[:, 772:900]), reads=["cst"], writes=["identb"])
    P.op("dve", lambda e: e.memset(Sf[:], 0.0), writes=["Sf"])
    P.op("dve", lambda e: e.memset(Sb_[:], 0.0), writes=["Sb"])
    P.op("pool", lambda e: e.memset(PTf[:], 0.0), writes=["PTf"])
    P.op("pool", lambda e: e.memset(PTb[:], 0.0), writes=["PTb"])
    P.dma("sp", "ldw", lambda e: e.dma_start(out=wstg, in_=win.rearrange("(kc p) n -> p kc n", p=128)), writes=["big"])
    P.op("pool", lambda e: e.tensor_copy(out=wb[:], in_=wstg), reads=["big"], writes=["wb"])
    P.dma("sp", "ldg", lambda e: e.dma_start(out=xt[0][:], in_=rows[0:1, :].to_broadcast([128, D])), writes=["xt0"])
    for k in range(2):
        P.dma("sp", "ldg", lambda e, k=k: [e.dma_start(out=Sh[k][:], in_=rows[1 + 2 * k:2 + 2 * k, :].to_broadcast([128, D])),
                                           e.dma_start(out=Gt[k][:], in_=rows[2 + 2 * k:3 + 2 * k, :].to_broadcast([128, D]))],
              writes=[f"Sh{k}", f"G{k}"], n=2)
        P.op("dve", lambda e, k=k: e.scalar_tensor_tensor(out=Gt[k][:], in0=Gt[k][:], scalar=1.0, in1=xt[0][:], op0=ALU.add, op1=ALU.mult),
             reads=[f"G{k}", "xt0"], writes=[f"G{k}"])
    P.op("act", lambda e: e.activation(out=lbt[:], in_=lbt[:], func=AF.Exp), reads=["lbt"], writes=["lbt"])
    for d in range(2):
        o = d * 384
        P.op("dve", lambda e, o=o, d=d: e.tensor_tensor(out=OML[:, d * 128:(d + 1) * 128], in0=lbt[:, o + 128:o + 256], in1=lbt[:, o + 256:o + 384], op=ALU.add),
             reads=["lbt"], writes=["OML"])
        P.op("dve", lambda e, o=o, d=d: e.tensor_tensor(out=LB[:, d * 128:(d + 1) * 128], in0=OML[:, d * 128:(d + 1) * 128], in1=lbt[:, o:o + 128], op=ALU.add),
             reads=["lbt", "OML"], writes=["LB"])
    P.op("dve", lambda e: e.reciprocal(out=LB[:], in_=LB[:]), reads=["LB"], writes=["LB"])
    P.op("dve", lambda e: e.tensor_tensor(out=OML[:], in0=OML[:], in1=LB[:], op=ALU.mult), reads=["OML", "LB"], writes=["OML"])
    P.op("dve", lambda e: e.tensor_scalar(out=LB[:], in0=OML[:], scalar1=-1.0, scalar2=1.0, op0=ALU.mult, op1=ALU.add),
         reads=["OML"], writes=["LB"])

    QS = 128.0 ** -0.5
    for n in range(NT):
        k = 1 if n < 2 else 0
        xi = n % 2
        xb = xt[xi]
        kx = f"xt{xi}"
        P.dma("sp", "ldx" + str(xi), lambda e, xb=xb, n=n: e.dma_start(out=xb[:], in_=xin[n * 128:(n + 1) * 128, :]), writes=[kx])
        emit_norm_stats(P, xb[:], kx, junk[:], ss, rs, epsb, D, "n1")
        P.op("dve", lambda e, xb=xb, k=k: e.scalar_tensor_tensor(out=xb[:], in0=xb[:], scalar=rs[:, 0:1], in1=Gt[k][:], op0=ALU.mult, op1=ALU.mult),
             reads=[kx, "n1rs", f"G{k}"], writes=[kx])
        P.op("pool", lambda e, xb=xb, k=k: e.tensor_tensor(out=hx[:], in0=xb[:], in1=Sh[k][:], op=ALU.add),
             reads=[kx, f"Sh{k}"], writes=["hx"])

        def tr(e):
            r = None
            for kc in range(16):
                dst = pTa if kc < 8 else pTb
                r = e.transpose(dst[:, (kc % 8) * 128:(kc % 8 + 1) * 128], hx[:, kc * 128:(kc + 1) * 128], identb[:])
            return r
        P.op("pe", tr, reads=["hx", "identb"], writes=["pTa", "pTb"])
        P.op("act", lambda e: e.activation(out=hxT[:, 0:1024], in_=pTa[:], func=AF.Copy), reads=["pTa"], writes=["hxTa"])
        P.op("dve", lambda e: e.tensor_copy(out=hxT[:, 1024:2048], in_=pTb[:]), reads=["pTb"], writes=["hxTb"])

        def mmz(e):
            r = None
            for kc in range(16):
                r = e.matmul(zps[:, :], lhsT=hxT[:, kc * 128:(kc + 1) * 128], rhs=wb[:, kc, :], start=(kc == 0), stop=(kc == 15))
            return r
        P.op("pe", mmz, reads=["hxTa", "hxTb", "wb"], writes=["zps"])
        P.op("act", lambda e: e.activation(out=sig[:], in_=zps[:, 0:256], func=AF.Sigmoid), reads=["zps"], writes=["sig"])
        P.op("act", lambda e: e.activation(out=qs[:], in_=zps[:, 384:512], func=AF.Silu), reads=["zps"], writes=["qs"])
        P.op("act", lambda e, n=n: e.activation(out=vb_all[:, n, :], in_=zps[:, 256:384], func=AF.Copy), reads=["zps"], writes=[f"vb{n}"])
        P.op("dve", lambda e: e.tensor_tensor(out=sig[:], in0=sig[:], in1=OML[:], op=ALU.mult), reads=["sig", "OML"], writes=["sig"])
        P.op("dve", lambda e: e.tensor_tensor(out=sig[:], in0=sig[:], in1=LB[:], op=ALU.add), reads=["sig", "LB"], writes=["sig"])
        P.op("act", lambda e: e.activation(out=logf[:], in_=sig[:], func=AF.Ln), reads=["sig"], writes=["logf"])
        P.op("dve", lambda e: e.tensor_scalar(out=kk[:], in0=sig[:], scalar1=-1.0, scalar2=1.0, op0=ALU.mult, op1=ALU.add),
             reads=["sig"], writes=["kk"])

        def mms(e):
            e.matmul(bcT[:, 0:130], lhsT=logf[:, 0:128], rhs=cst[:, 0:130], start=True, stop=True)
            e.matmul(bcT[:, 130:260], lhsT=logf[:, 128:256], rhs=cst[:, 130:260], start=True, stop=True)
            e.matmul(rev[:, 0:128], lhsT=cst[:, 260:388], rhs=logf[:, 0:128], start=True, stop=True)
            return e.matmul(rev[:, 128:256], lhsT=cst[:, 388:516], rhs=logf[:, 128:256], start=True, stop=True)
        P.op("pe", mms, reads=["logf", "cst"], writes=["bcT", "rev"])

        def mmt(e):
            e.matmul(tq[:, 0:128], lhsT=qs[:], rhs=cst[:, 772:900], start=True, stop=True)
            e.matmul(tq[:, 128:256], lhsT=kk[:, 0:128], rhs=cst[:, 772:900], start=True, stop=True)
            return e.matmul(tq[:, 256:384], lhsT=kk[:, 128:256], rhs=cst[:, 772:900], start=True, stop=True)
        P.op("pe", mmt, reads=["qs", "kk", "cst"], writes=["tq"])
        for d in range(2):
            o = d * 130
            P.op("act", lambda e, d=d, o=o: e.activation(out=Ep[:, d * 128:(d + 1) * 128], in_=bcT[:, o:o + 128], func=AF.Exp),
                 reads=["bcT"], writes=[f"Ep{d}"])
            P.op("act", lambda e, d=d, o=o: e.activation(out=En[:, d * 128:(d + 1) * 128], in_=bcT[:, o:o + 128], func=AF.Exp, scale=-1.0),
                 reads=["bcT"], writes=[f"En{d}"])
        P.op("act", lambda e: e.activation(out=em[:, 0:2], in_=bcT[:, 128:130], func=AF.Exp), reads=["bcT"], writes=["emf"])
        P.op("act", lambda e, n=n: e.activation(out=emb_all[:, n, :], in_=bcT[:, 258:260], func=AF.Exp), reads=["bcT"], writes=[f"emb{n}"])
        P.op("act", lambda e: e.activation(out=Erev[:], in_=rev[:, 0:256], func=AF.Exp), reads=["rev"], writes=["Erev"])
        P.op("dve", lambda e: e.scalar_tensor_tensor(out=qtf[:], in0=tq[:, 0:128], scalar=QS, in1=Ep[:, 0:128], op0=ALU.mult, op1=ALU.mult),
             reads=["tq", "Ep0"], writes=["qtf"])
        P.op("dve", lambda e, n=n: e.scalar_tensor_tensor(out=qtb_all[:, n, :], in0=tq[:, 0:128], scalar=QS, in1=Ep[:, 128:256], op0=ALU.mult, op1=ALU.mult),
             reads=["tq", "Ep1"], writes=[f"qtb{n}"])
        P.op("dve", lambda e: e.tensor_tensor(out=ktf[:], in0=tq[:, 128:256], in1=En[:, 0:128], op=ALU.mult), reads=["tq", "En0"], writes=["ktf"])
        P.op("dve", lambda e: e.tensor_tensor(out=ktb[:], in0=tq[:, 256:384], in1=En[:, 128:256], op=ALU.mult), reads=["tq", "En1"], writes=["ktb"])
        P.op("pool", lambda e: e.tensor_tensor(out=kdf[:], in0=kk[:, 0:128], in1=Erev[:, 0:128], op=ALU.mult), reads=["kk", "Erev"], writes=["kdf"])
        P.op("pool", lambda e, n=n: e.tensor_tensor(out=kdb_all[:, n, :], in0=kk[:, 128:256], in1=Erev[:, 128:256], op=ALU.mult),
             reads=["kk", "Erev"], writes=[f"kdb{n}"])
        if n >= 2:
            def mmsc(e, n=n):
                e.matmul(sc[0:64, 0:128], lhsT=ktf[:, 0:64], rhs=qtf[:, 0:128], start=True, stop=True)
                e.matmul(sc[64:128, 64:128], lhsT=ktf[:, 64:128], rhs=qtf[:, 64:128], start=True, stop=True)
                e.matmul(sc[64:128, 128:256], lhsT=ktb[:, 64:128], rhs=qtb_all[:, n, 0:128], start=True, stop=True)
                return e.matmul(sc[0:64, 128:192], lhsT=ktb[:, 0:64], rhs=qtb_all[:, n, 0:64], start=True, stop=True)
            P.op("pe", mmsc, reads=["ktf", "qtf", "ktb", f"qtb{n}"], writes=["scf", "scb"])
            BIG = 1e30
            P.op("dve", lambda e: e.scalar_tensor_tensor(out=PTf[0:64, :], in0=sc[0:64, 0:128], scalar=BIG, in1=cst[0:64, 516:644], op0=ALU.min, op1=ALU.mult),
                 reads=["scf", "cst"], writes=["PTf"])
            P.op("dve", lambda e: e.scalar_tensor_tensor(out=PTf[64:128, 64:128], in0=sc[64:128, 64:128], scalar=BIG, in1=cst[64:128, 580:644], op0=ALU.min, op1=ALU.mult),
                 reads=["scf", "cst"], writes=["PTf"])
            P.op("dve", lambda e: e.scalar_tensor_tensor(out=PTb[64:128, :], in0=sc[64:128, 128:256], scalar=BIG, in1=cst[64:128, 644:772], op0=ALU.min, op1=ALU.mult),
                 reads=["scb", "cst"], writes=["PTb"])
            P.op("dve", lambda e: e.scalar_tensor_tensor(out=PTb[0:64, 0:64], in0=sc[0:64, 128:192], scalar=BIG, in1=cst[0:64, 644:708], op0=ALU.min, op1=ALU.mult),
                 reads=["scb", "cst"], writes=["PTb"])
            P.op("dve", lambda e: e.tensor_scalar(out=Spf[:], in0=Sf[:], scalar1=em[:, 0:1], scalar2=None, op0=ALU.mult),
                 reads=["Sf", "emf"], writes=["Spf"])

            def mmo(e, n=n):
                e.matmul(ops[:, 0:128], lhsT=PTf[:], rhs=vb_all[:, n, :], start=True, stop=False)
                e.matmul(ops[:, 0:128], lhsT=PTb[:], rhs=vb_all[:, n, :], start=False, stop=False)
                return e.matmul(ops[:, 0:128], lhsT=qtf[:], rhs=Spf[:], start=False, stop=True)
            P.op("pe", mmo, reads=["PTf", "PTb", f"vb{n}", "qtf", "Spf"], writes=["ops1"])
            P.op("act", lambda e, n=n: e.activation(out=oacc[:, n - 2, :], in_=ops[:, 0:128], func=AF.Copy), reads=["ops1"], writes=["big"])
        P.op("pe", lambda e, n=n: e.matmul(sc[:, 256:384], lhsT=kdf[:], rhs=vb_all[:, n, :], start=True, stop=True),
             reads=["kdf", f"vb{n}"], writes=["dSf"])
        P.op("dve", lambda e: e.scalar_tensor_tensor(out=Sf[:], in0=Sf[:], scalar=em[:, 1:2], in1=sc[:, 256:384], op0=ALU.mult, op1=ALU.add),
             reads=["Sf", "emf", "dSf"], writes=["Sf"])

    order = [1, 0] + list(range(NT - 1, 1, -1))
    for it, n in enumerate(order):
        if n >= 2:
            P.op("dve", lambda e, n=n: e.tensor_scalar(out=Spb[:], in0=Sb_[:], scalar1=emb_all[:, n, 0:1], scalar2=None, op0=ALU.mult),
                 reads=["Sb", f"emb{n}"], writes=["Spb"])
            P.op("pe", lambda e, n=n: e.matmul(ops[:, 128:256], lhsT=qtb_all[:, n, :], rhs=Spb[:], start=True, stop=True),
                 reads=[f"qtb{n}", "Spb"], writes=["ops2"])
            P.op("dve", lambda e, n=n: e.tensor_tensor(out=ofin[:], in0=oacc[:, n - 2, :], in1=ops[:, 128:256], op=ALU.add),
                 reads=["big", "ops2"], writes=["ofin"])
            emit_norm_stats(P, ofin[:], "ofin", junk2[:], ss2, rs2, epsb, 128, "n2")
            ob = obuf[it % 2]
            ko = f"obuf{it % 2}"
            P.op("dve", lambda e, ob=ob: e.scalar_tensor_tensor(out=ob[:], in0=ofin[:], scalar=rs2[:, 0:1], in1=HG[:], op0=ALU.mult, op1=ALU.mult),
                 reads=["ofin", "n2rs", "HG"], writes=[ko])
            P.dma("sp", "sto" + str(it % 2), lambda e, ob=ob, n=n: e.dma_start(out=outd[(n - 2) * 128:(n - 1) * 128, :], in_=ob[:]), reads=[ko])
        P.op("pe", lambda e, n=n: e.matmul(sc[:, 384:512], lhsT=kdb_all[:, n, :], rhs=vb_all[:, n, :], start=True, stop=True),
             reads=[f"kdb{n}", f"vb{n}"], writes=["dSb"])
        P.op("dve", lambda e, n=n: e.scalar_tensor_tensor(out=Sb_[:], in0=Sb_[:], scalar=emb_all[:, n, 1:2], in1=sc[:, 384:512], op0=ALU.mult, op1=ALU.add),
             reads=["Sb", f"emb{n}", "dSb"], writes=["Sb"])
    return C.done()


def run_hgrn(x, ctx, mods, norm_gain, w_in, lb_raw, hg_norm_gain):
    nc = build_hgrn()
    xin = np.ascontiguousarray(np.concatenate([ctx[0], x[0]], axis=0))
    mx0, mc0 = mods[0, 0], mods[1, 0]
    rows = np.stack([norm_gain[0, 0], mx0[0:D], mx0[D:2 * D], mc0[0:D], mc0[D:2 * D]], 0).astype(np.float32)
    cst = hgrn_consts()
    ims = []
    for j in range(NCORES):
        cs = [slice(b * 1024 + j * 128, b * 1024 + (j + 1) * 128) for b in range(4)]
        win = np.ascontiguousarray(np.concatenate([w_in[0][:, s] for s in cs], axis=1))
        lbr = np.ascontiguousarray(lb_raw[:, :, j * 128:(j + 1) * 128]).reshape(1, 768)
        hgg = np.ascontiguousarray(hg_norm_gain[0, j * 128:(j + 1) * 128]).reshape(1, 128)
        ims.append({"xin": xin, "win": win, "rows": rows, "lbr": lbr, "hgg": hgg, "cst": cst})
    res = run_bass_kernel_spmd(nc, ims, core_ids=list(range(NCORES)))
    return np.concatenate([res.results[j]["o_n"] for j in range(NCORES)], axis=1)


def build_token(stage):
    C = Ctx()
    P = C.P
    isC = stage == "C"
    NR = 13 if isC else 7
    xres = C.din("xres", [1024, D])
    rows = C.din("rows", [NR, D])
    wmix = C.din("wmix", [D, D])
    w1 = C.din("w1", [D, 8192])
    w2 = C.din("w2", [8192, D])
    x1d = C.dint("x1d", [1024, D])
    if isC:
        oTd = C.din("oT", [1024, 1024])
        wguv = C.din("wguv", [D, 3072])
        sgwT = C.din("sgwT", [8, 128, 128])
        sgbd = C.din("sgb", [1, 1024])
        lnd = C.din("lnd", [2, 1024])
        csd = C.din("csd", [512, 1024])
        identd = C.din("identd", [128, 128])
        xoutd = C.dout("x2", [1024, D])
        abd = C.dout("AB", [1024, 4096], BF16)
    else:
        ytd = C.din("YT", [D, 1024], BF16)
        identd = C.din("identd", [128, 128])
        xoutd = C.dout("xo", [1024, D])

    R1 = C.sb("R1", [128, 16384])
    R2 = C.sb("R2", [128, 16, 1024], BF16)
    stgA = C.sb("stgA", [128, 4096])
    wbA = [C.sb(f"wbA{i}", [128, 4096], BF16) for i in range(2)]
    SC = C.sb("SC", [128, 12288])
    rl = [C.sb(f"rl{i}", [128, 512], BF16) for i in range(2)]
    identf = C.sb("identf", [128, 128])
    identb = C.sb("identb", [128, 128], BF16)
    epsb = C.sb("epsb", [128, 1])
    ss = C.sb("ss", [128, 1]); rs = C.sb("rs", [128, 1])
    msum = C.sb("msum", [128, 1]); mean = C.sb("mean", [128, 1]); m2 = C.sb("m2", [128, 1])
    ones32 = C.sb("ones32", [1, 128]); sgb_sb = C.sb("sgb_sb", [1, 1024])
    wsT = C.sb("wsT", [128, 8, 128], BF16)

    yacc = R1[:, :].rearrange("p (t n) -> p t n", t=8)
    R1b = R1[:, :].bitcast(BF16)
    hxT = R1b[:, 0:16384].rearrange("p (k t) -> p k t", k=16)
    vgf = R1[:, 8192:16384].rearrange("p (t n) -> p t n", t=8)
    bc = [SC[:, i * 2048:(i + 1) * 2048] for i in range(3)]
    xt = SC[:, 6144:8192]
    SCb = SC[:, :].bitcast(BF16)
    hx = SCb[:, 16384:18432]
    junk = SCb[:, 18432:20480]
    stgB = SC[:, 0:4096].rearrange("p (b n) -> p b n", b=2)
    wbB = [SCb[:, 8192 + i * 4096:8192 + (i + 1) * 4096].rearrange("p (b n) -> p b n", b=2) for i in range(2)]
    uT = [SCb[:, 16384 + i * 2048:16384 + (i + 1) * 2048].rearrange("p (b n) -> p b n", b=2) for i in range(2)]
    for a, b in [("stgB", "bc0"), ("stgB", "bc1"), ("wbB0", "bc2"), ("wbB1", "xt"), ("uT0", "hx"), ("uT1", "tjunk")]:
        P.alias(a, b)
    V16 = C.sb("V16", [128, 8192], BF16)
    vn = V16[:, :].rearrange("p (t n) -> p t n", t=8)
    CSb = V16[:, 0:4096].rearrange("p (k n) -> p k n", k=4)
    abt = V16[:, 4096:8192]
    for t_ in range(4):
        P.alias("CSb", ("vn", t_))
        P.alias("abt", ("vn", 4 + t_))
    lnt = SC[:, 10240:12288]
    ot = stgA[:, 0:1024]

    pT = [C.ps("pTa", [128, 1024], BF16), C.ps("pTb", [128, 1024], BF16)]
    pA = [C.ps(f"pA{i}", [128, 512]) for i in range(2)]
    hp = [C.ps(f"hp{i}", [128, 512]) for i in range(2)]
    yp = [C.ps(f"yp{i}", [128, 512]) for i in range(2)]

    cnt = {"pA": 0, "hp": 0, "yp": 0, "rl": 0, "ev": 0}

    def nxt(k):
        v = cnt[k]
        cnt[k] += 1
        return v % 2

    P.dma("sp", "ldi", lambda e: e.dma_start(out=identf[:], in_=identd[:, :]), writes=["identf"])
    P.op("dve", lambda e: e.tensor_copy(out=identb[:], in_=identf[:]), reads=["identf"], writes=["identb"])
    P.op("pool", lambda e: e.memset(epsb[:], EPS), writes=["epsb"])
    P.op("pool", lambda e: e.memset(ones32[:], 1.0), writes=["ones32"])

    def bload(i, r):
        P.dma("sp", f"ldbc{i}", lambda e: e.dma_start(out=bc[i], in_=rows[r:r + 1, :].to_broadcast([128, D])), writes=[f"bc{i}"])

    def prep_mod(iG, iS, r_gain, r_shift, r_scale):
        bload(iG, r_scale)
        P.dma("sp", "ldxt", lambda e: e.dma_start(out=xt, in_=rows[r_gain:r_gain + 1, :].to_broadcast([128, D])), writes=["xt"])
        P.op("dve", lambda e: e.scalar_tensor_tensor(out=bc[iG], in0=bc[iG], scalar=1.0, in1=xt, op0=ALU.add, op1=ALU.mult),
             reads=[f"bc{iG}", "xt"], writes=[f"bc{iG}"])
        bload(iS, r_shift)

    def prep_gate(i, r_gate, r_gpost):
        bload(i, r_gate)
        P.dma("sp", "ldxt", lambda e: e.dma_start(out=xt, in_=rows[r_gpost:r_gpost + 1, :].to_broadcast([128, D])), writes=["xt"])
        P.op("dve", lambda e: e.tensor_tensor(out=bc[i], in0=bc[i], in1=xt, op=ALU.mult), reads=[f"bc{i}", "xt"], writes=[f"bc{i}"])

    def stats(in_ap, kin, nfeat):
        emit_norm_stats(P, in_ap, kin, junk, ss, rs, epsb, nfeat, "t")

    def modulate_to(src, ksrc, tmp, ktmp, iG, iS, dstT, kdst, t):
        P.op("dve", lambda e: e.scalar_tensor_tensor(out=tmp, in0=src, scalar=rs[:, 0:1], in1=bc[iG], op0=ALU.mult, op1=ALU.mult),
             reads=[ksrc, "trs", f"bc{iG}"], writes=[ktmp])
        P.op("pool", lambda e: e.tensor_tensor(out=hx, in0=tmp, in1=bc[iS], op=ALU.add), reads=[ktmp, f"bc{iS}"], writes=["hx"])

        def tr(e):
            r = None
            for kc in range(16):
                r = e.transpose(pT[kc // 8][:, (kc % 8) * 128:(kc % 8 + 1) * 128], hx[:, kc * 128:(kc + 1) * 128], identb[:])
            return r
        P.op("pe", tr, reads=["hx", "identb"], writes=["pTa", "pTb"])
        P.op("act", lambda e: e.activation(out=dstT[:, 0:8, t * 128:(t + 1) * 128], in_=pT[0][:, :].rearrange("p (k t) -> p k t", k=8), func=AF.Copy),
             reads=["pTa"], writes=[(kdst, k) for k in range(8)])
        P.op("dve", lambda e: e.tensor_copy(out=dstT[:, 8:16, t * 128:(t + 1) * 128], in_=pT[1][:, :].rearrange("p (k t) -> p k t", k=8)),
             reads=["pTb"], writes=[(kdst, k) for k in range(8, 16)])

    def stream_w(src_view, b, nelem_shape):
        a, n = nelem_shape
        sv = stgA[:, 0:a * n].rearrange("p (a n) -> p a n", a=a)
        wv = wbA[b][:, 0:a * n].rearrange("p (a n) -> p a n", a=a)
        P.dma("sp", "ldwA", lambda e: e.dma_start(out=sv, in_=src_view), writes=["stgA"])
        P.op("pool", lambda e: e.tensor_copy(out=wv, in_=sv), reads=["stgA"], writes=[f"wbA{b}"])
        return wv

    wcnt = [0]

    def linear_tok(aT, ka, wd, ncols, evac):
        wv_d = wd.rearrange("(kc p) n -> p kc n", p=128)
        for nq in range(ncols // 256):
            b = wcnt[0] % 2
            wcnt[0] += 1
            wv = stream_w(wv_d[:, :, nq * 256:(nq + 1) * 256], b, (16, 256))
            for t in range(8):
                i = nxt("pA")

                def mm(e, t=t, i=i, wv=wv):
                    r = None
                    for kc in range(16):
                        r = e.matmul(pA[i][:, 0:256], lhsT=aT[:, kc, t * 128:(t + 1) * 128], rhs=wv[:, kc, :], start=(kc == 0), stop=(kc == 15))
                    return r
                P.op("pe", mm, reads=[(ka, k) for k in range(16)] + [f"wbA{b}"], writes=[f"pA{i}"])
                evac(t, nq, pA[i][:, 0:256], f"pA{i}")

    def evac_copy_to_yacc(t, nq, ps, kps):
        if nxt("ev") == 0:
            P.op("act", lambda e: e.activation(out=yacc[:, t, nq * 256:(nq + 1) * 256], in_=ps, func=AF.Copy), reads=[kps], writes=[("R1", t)])
        else:
            P.op("dve", lambda e: e.tensor_copy(out=yacc[:, t, nq * 256:(nq + 1) * 256], in_=ps), reads=[kps], writes=[("R1", t)])

    if isC:
        prep_mod(0, 1, 0, 1, 2)
        for t in range(8):
            P.dma("sp", "ldx", lambda e, t=t: e.dma_start(out=xt, in_=xres[t * 128:(t + 1) * 128, :]), writes=["xt"])
            stats(xt, "xt", D)
            modulate_to(xt, "xt", xt, "xt", 0, 1, hxT, "hxT", t)
        for k in range(16):
            P.alias(("hxT", k), ("R1", k // 4))
        P.dma("sp", "ldsm", lambda e: [e.dma_start(out=sgb_sb[:], in_=sgbd[0:1, :]),
                                       e.dma_start(out=lnt[:, 0:1024], in_=lnd[0:1, :].to_broadcast([128, 1024])),
                                       e.dma_start(out=lnt[:, 1024:2048], in_=lnd[1:2, :].to_broadcast([128, 1024]))],
              writes=["sgb_sb", "lnt"], n=3)
        sv = stgA[:, 0:1024].rearrange("p (g t) -> p g t", g=8)
        P.dma("sp", "ldwA", lambda e, sv=sv: e.dma_start(out=sv, in_=sgwT.rearrange("g s t -> s g t")), writes=["stgA"])
        P.op("pool", lambda e, sv=sv: e.tensor_copy(out=wsT[:], in_=sv), reads=["stgA"], writes=["wsT"])
        def evac_v(t, nq, ps, kps):
            P.op("act", lambda e: e.activation(out=vgf[:, t, nq * 256:(nq + 1) * 256], in_=ps, func=AF.Gelu_apprx_tanh), reads=[kps], writes=[("R1", 4 + t // 2)])
        linear_tok(hxT, "hxT", wguv[:, 2048:3072], 1024, evac_v)
        for t in range(8):
            kv = ("R1", 4 + t // 2)
            P.op("dve", lambda e, t=t: e.reduce_sum(out=msum[:], in_=vgf[:, t, :], axis=mybir.AxisListType.X), reads=[kv], writes=["msum"])
            P.op("pool", lambda e: e.memset(ss[:], 0.0), writes=["tss"])
            P.op("act", lambda e, t=t: e.activation(out=junk[:, 0:1024], in_=vgf[:, t, :], func=AF.Square, accum_out=ss[:]), reads=[kv], writes=["tjunk", "tss"])
            P.op("dve", lambda e: e.tensor_scalar(out=mean[:], in0=msum[:], scalar1=1.0 / 1024, scalar2=None, op0=ALU.mult), reads=["msum"], writes=["mean"])
            P.op("dve", lambda e: e.tensor_tensor(out=m2[:], in0=mean[:], in1=mean[:], op=ALU.mult), reads=["mean"], writes=["m2"])
            P.op("dve", lambda e: e.scalar_tensor_tensor(out=m2[:], in0=ss[:], scalar=1.0 / 1024, in1=m2[:], op0=ALU.mult, op1=ALU.subtract),
                 reads=["tss", "m2"], writes=["m2"])
            P.op("act", lambda e: e.activation(out=rs[:], in_=m2[:], func=AF.Sqrt, bias=epsb[:]), reads=["m2"], writes=["trs"])
            P.op("dve", lambda e: e.reciprocal(out=rs[:], in_=rs[:]), reads=["trs"], writes=["trs"])
            P.op("dve", lambda e, t=t: e.tensor_scalar(out=vgf[:, t, :], in0=vgf[:, t, :], scalar1=mean[:, 0:1], scalar2=rs[:, 0:1], op0=ALU.subtract, op1=ALU.mult),
                 reads=[kv, "mean", "trs"], writes=[kv])
            P.op("dve", lambda e, t=t: e.tensor_tensor(out=vgf[:, t, :], in0=vgf[:, t, :], in1=lnt[:, 0:1024], op=ALU.mult), reads=[kv, "lnt"], writes=[kv])
            P.op("pool", lambda e, t=t: e.tensor_tensor(out=vn[:, t, :], in0=vgf[:, t, :], in1=lnt[:, 1024:2048], op=ALU.add), reads=[kv, "lnt"], writes=[("vn", t)])
        wg_d = wguv[:, 0:2048].rearrange("(kc p) n -> p kc n", p=128)
        for cb in range(8):
            b = wcnt[0] % 2
            wcnt[0] += 1
            wv = stream_w(wg_d[:, :, cb * 256:(cb + 1) * 256], b, (16, 256))
            for blk in range(2):
                gb = cb * 2 + blk
                for half in range(2):
                    i = nxt("pA")

                    def mm(e, blk=blk, half=half, i=i, wv=wv):
                        r = None
                        for kc in range(16):
                            r = e.matmul(pA[i][:, :], lhsT=wv[:, kc, blk * 128:(blk + 1) * 128], rhs=hxT[:, kc, half * 512:(half + 1) * 512],
                                         start=(kc == 0), stop=(kc == 15))
                        return r
                    P.op("pe", mm, reads=[("hxT", k) for k in range(16)] + [f"wbA{b}"], writes=[f"pA{i}"])
                    fn = AF.Silu if gb < 8 else AF.Gelu_apprx_tanh
                    P.op("act", lambda e, gb=gb, half=half, i=i, fn=fn: e.activation(out=R2[:, gb, half * 512:(half + 1) * 512], in_=pA[i][:, :], func=fn),
                         reads=[f"pA{i}"], writes=[("R2", gb)])
        for t in range(8):
            for gq in range(2):
                i = nxt("pA")

                def mm(e, t=t, gq=gq, i=i):
                    r = None
                    for gg in range(4):
                        g = gq * 4 + gg
                        e.matmul(pA[i][:, gg * 128:(gg + 1) * 128], lhsT=vn[:, t, g * 128:(g + 1) * 128], rhs=wsT[:, g, :], start=True, stop=False)
                        r = e.matmul(pA[i][:, gg * 128:(gg + 1) * 128], lhsT=ones32[0:1, :], rhs=sgb_sb[0:1, g * 128:(g + 1) * 128], start=False, stop=True)
                    return r
                P.op("pe", mm, reads=[("vn", t), "wsT", "ones32", "sgb_sb"], writes=[f"pA{i}"])
                ks = [("R2", 8 + gq * 4 + gg) for gg in range(4)]
                P.op("dve", lambda e, t=t, gq=gq, i=i: e.tensor_tensor(out=R2[:, 8 + gq * 4:12 + gq * 4, t * 128:(t + 1) * 128],
                                                                       in0=pA[i][:, :].rearrange("p (g t) -> p g t", g=4),
                                                                       in1=R2[:, 8 + gq * 4:12 + gq * 4, t * 128:(t + 1) * 128], op=ALU.mult),
                     reads=[f"pA{i}"] + ks, writes=ks)
        for h in range(8):
            P.dma("sp", "ldwA", lambda e, h=h: e.dma_start(out=ot, in_=oTd[h * 128:(h + 1) * 128, :]), writes=["stgA"])
            P.op("dve", lambda e, h=h: e.tensor_tensor(out=R2[:, h, :], in0=ot, in1=R2[:, h, :], op=ALU.mult), reads=["stgA", ("R2", h)], writes=[("R2", h)])
    else:
        P.dma("sp", "ldyt", lambda e: e.dma_start(out=R2[:], in_=ytd.rearrange("(kc p) t -> p kc t", p=128)), writes=[("R2", k) for k in range(16)])

    linear_tok(R2, "R2", wmix, D, evac_copy_to_yacc)

    def post(iGG, res_src, ksrc_d, res_dst, kdst_d, nextmod, dstT, kdst):
        for t in range(8):
            ky = ("R1", t)
            stats(yacc[:, t, :], ky, D)
            P.dma("sp", "ldx", lambda e, t=t: e.dma_start(out=xt, in_=res_src[t * 128:(t + 1) * 128, :]), reads=[(ksrc_d, t)], writes=["xt"])
            P.op("dve", lambda e, t=t: e.scalar_tensor_tensor(out=yacc[:, t, :], in0=yacc[:, t, :], scalar=rs[:, 0:1], in1=bc[iGG], op0=ALU.mult, op1=ALU.mult),
                 reads=[ky, "trs", f"bc{iGG}"], writes=[ky])
            P.op("pool", lambda e, t=t: e.tensor_tensor(out=xt, in0=xt, in1=yacc[:, t, :], op=ALU.add), reads=["xt", ky], writes=["xt"])
            P.dma("sp", "stx", lambda e, t=t: e.dma_start(out=res_dst[t * 128:(t + 1) * 128, :], in_=xt), reads=["xt"], writes=[(kdst_d, t)])
            if nextmod is not None:
                stats(xt, "xt", D)
                modulate_to(xt, "xt", yacc[:, t, :], ky, nextmod[0], nextmod[1], dstT, kdst, t)

    B0 = 3 if isC else 0
    prep_gate(0, B0 + 0, B0 + 1)
    prep_mod(1, 2, B0 + 2, B0 + 3, B0 + 4)
    post(0, xres, "d_xres", x1d, "d_x1", (1, 2), R2, "R2")

    w1v = w1.rearrange("(kc p) n -> p kc n", p=128)
    NG = 32

    def prefetch(g):
        b = g % 2
        sv = stgA[:, :].rearrange("p (a n) -> p a n", a=16)
        wv = wbA[b][:, :].rearrange("p (a n) -> p a n", a=16)
        P.dma("sp", "ldwA", lambda e: e.dma_start(out=sv, in_=w1v[:, :, g * 256:(g + 1) * 256]), writes=["stgA"])
        P.op("pool", lambda e: e.tensor_copy(out=wv, in_=sv), reads=["stgA"], writes=[f"wbA{b}"])
        P.dma("sp", "ldwB", lambda e: e.dma_start(out=stgB, in_=w2[g * 256:(g + 1) * 256, :].rearrange("(b p) n -> p b n", p=128)), writes=["stgB"])
        P.op("act", lambda e: e.activation(out=wbB[b], in_=stgB, func=AF.Copy), reads=["stgB"], writes=[f"wbB{b}"])

    def mm1(g):
        b = g % 2
        wv = wbA[b][:, :].rearrange("p (a n) -> p a n", a=16)
        for blk in range(2):
            for half in range(2):
                i = nxt("hp")

                def mm(e, blk=blk, half=half, i=i):
                    r = None
                    for kc in range(16):
                        r = e.matmul(hp[i][:, :], lhsT=wv[:, kc, blk * 128:(blk + 1) * 128], rhs=R2[:, kc, half * 512:(half + 1) * 512],
                                     start=(kc == 0), stop=(kc == 15))
                    return r
                P.op("pe", mm, reads=[("R2", k) for k in range(16)] + [f"wbA{b}"], writes=[f"hp{i}"])
                r_ = nxt("rl")
                P.op("act", lambda e, i=i, r_=r_: e.activation(out=rl[r_][:], in_=hp[i][:, :], func=AF.Relu), reads=[f"hp{i}"], writes=[f"rl{r_}"])
                P.op("pool", lambda e, blk=blk, half=half, r_=r_: e.tensor_tensor(out=uT[b][:, blk, half * 512:(half + 1) * 512], in0=rl[r_][:], in1=rl[r_][:], op=ALU.mult),
                     reads=[f"rl{r_}"], writes=[f"uT{b}"])

    def mm2(g):
        b = g % 2
        for t in range(8):
            for nq in range(4):
                i = nxt("yp")

                def mm(e, t=t, nq=nq, i=i):
                    e.matmul(yp[i][:, :], lhsT=uT[b][:, 0, t * 128:(t + 1) * 128], rhs=wbB[b][:, 0, nq * 512:(nq + 1) * 512], start=True, stop=False)
                    return e.matmul(yp[i][:, :], lhsT=uT[b][:, 1, t * 128:(t + 1) * 128], rhs=wbB[b][:, 1, nq * 512:(nq + 1) * 512], start=False, stop=True)
                P.op("pe", mm, reads=[f"uT{b}", f"wbB{b}"], writes=[f"yp{i}"])
                ky = ("R1", t)
                if g == 0:
                    P.op("dve", lambda e, t=t, nq=nq, i=i: e.tensor_copy(out=yacc[:, t, nq * 512:(nq + 1) * 512], in_=yp[i][:, :]), reads=[f"yp{i}"], writes=[ky])
                else:
                    P.op("dve", lambda e, t=t, nq=nq, i=i: e.tensor_tensor(out=yacc[:, t, nq * 512:(nq + 1) * 512], in0=yacc[:, t, nq * 512:(nq + 1) * 512],
                                                                          in1=yp[i][:, :], op=ALU.add), reads=[f"yp{i}", ky], writes=[ky])

    prefetch(0)
    mm1(0)
    for g in range(NG):
        if g + 1 < NG:
            prefetch(g + 1)
            mm1(g + 1)
        mm2(g)

    prep_gate(0, B0 + 5, B0 + 6)
    if isC:
        prep_mod(1, 2, 10, 11, 12)
        post(0, x1d, "d_x1", xoutd, "d_xout", (1, 2), R2, "R2")
        sv = stgA[:, :].rearrange("p (a n) -> p a n", a=4)
        P.dma("sp", "ldwA", lambda e: e.dma_start(out=sv, in_=csd.rearrange("(kc p) n -> p kc n", p=128)), writes=["stgA"])
        P.op("pool", lambda e: e.tensor_copy(out=CSb, in_=sv), reads=["stgA"], writes=["CSb"])
        for t in range(8):
            for g in range(4):
                for half in range(2):
                    i = nxt("pA")

                    def mm(e, t=t, g=g, half=half, i=i):
                        r = None
                        for kc in range(4):
                            r = e.matmul(pA[i][:, :], lhsT=R2[:, g * 4 + kc, t * 128:(t + 1) * 128], rhs=CSb[:, kc, half * 512:(half + 1) * 512],
                                         start=(kc == 0), stop=(kc == 3))
                        return r
                    P.op("pe", mm, reads=[("R2", k) for k in range(g * 4, g * 4 + 4)] + ["CSb"], writes=[f"pA{i}"])
                    dst = abt[:, g * 1024 + half * 512:g * 1024 + (half + 1) * 512]
                    if nxt("ev") == 0:
                        P.op("act", lambda e, dst=dst, i=i: e.activation(out=dst, in_=pA[i][:, :], func=AF.Copy), reads=[f"pA{i}"], writes=["abt"])
                    else:
                        P.op("dve", lambda e, dst=dst, i=i: e.tensor_copy(out=dst, in_=pA[i][:, :]), reads=[f"pA{i}"], writes=["abt"])
            P.dma("sp", "stab", lambda e, t=t: e.dma_start(out=abd[t * 128:(t + 1) * 128, :], in_=abt), reads=["abt"])
    else:
        post(0, x1d, "d_x1", xoutd, "d_xout", None, None, None)
    return C.done()


def build_seqdft():
    C = Ctx()
    P = C.P
    ain = C.din("ain", [SEQ, 256], BF16)
    bin_ = C.din("bin", [SEQ, 256], BF16)
    w1d = C.din("w1d", [128, 128], BF16)
    wcsd = C.din("wcsd", [128, 64 * 2 * 128], BF16)
    yd = C.dout("y", [SEQ, 256], BF16)
    Zt = C.sb("Zt", [128, 128 * 256], BF16)
    Gs = C.sb("Gs", [128, 256, 128], BF16)
    W1 = C.sb("W1", [128, 128], BF16)
    Wcs = C.sb("Wcs", [128, 64, 2, 128], BF16)
    Ych = [C.sb(f"Ych{i}", [128, 8, 256], BF16) for i in range(2)]
    ps1 = [C.ps(f"ps1_{i}", [128, 512]) for i in range(2)]
    ps3 = [C.ps(f"ps3_{i}", [128, 512]) for i in range(2)]
    Z3 = Zt[:, :].rearrange("p (s c) -> p s c", c=256)
    av = ain.rearrange("(s1 s2) c -> s1 (s2 c)", s2=128)
    bv = bin_.rearrange("(s1 s2) c -> s1 (s2 c)", s2=128)
    P.dma("sp", "ldw", lambda e: [e.dma_start(out=W1[:], in_=w1d[:, :]),
                                  e.dma_start(out=Wcs[:, :, :, :].rearrange("p a b c -> p (a b c)"), in_=wcsd[:, :])], writes=["W1", "Wcs"], n=2)
    for q in range(4):
        sl = slice(q * 8192, (q + 1) * 8192)
        P.dma("sp", "ldz", lambda e, sl=sl: [e.dma_start(out=Zt[0:64, sl], in_=av[:, sl]), e.dma_start(out=Zt[64:128, sl], in_=bv[:, sl])],
              writes=[("Zt", q)], n=2)
    for cg in range(64):
        i = cg % 2

        def mm(e, cg=cg, i=i):
            r = None
            for cc in range(4):
                col = cg * 4 + cc
                r = e.matmul(ps1[i][:, cc * 128:(cc + 1) * 128], lhsT=Z3[:, :, col], rhs=W1[:], start=True, stop=True)
            return r
        P.op("pe", mm, reads=[("Zt", q) for q in range(4)] + ["W1"], writes=[f"ps1_{i}"])
        dst = Gs[:, cg * 4:(cg + 1) * 4, :]
        src = ps1[i][:, :].rearrange("p (c n) -> p c n", c=4)
        if cg % 2 == 0:
            P.op("act", lambda e, dst=dst, src=src: e.activation(out=dst, in_=src, func=AF.Copy), reads=[f"ps1_{i}"], writes=[("Gs", cg)])
        else:
            P.op("dve", lambda e, dst=dst, src=src: e.tensor_copy(out=dst, in_=src), reads=[f"ps1_{i}"], writes=[("Gs", cg)])
    ydv = yd.rearrange("(t2 t1) c -> t2 t1 c", t1=64)
    for t1 in range(64):
        i = t1 % 2

        def mm(e, t1=t1, i=i):
            e.matmul(ps3[i][:, 0:256], lhsT=Wcs[:, t1, 0, :], rhs=Gs[:, :, t1], start=True, stop=False)
            return e.matmul(ps3[i][:, 0:256], lhsT=Wcs[:, t1, 1, :], rhs=Gs[:, :, 64 + t1], start=False, stop=True)
        P.op("pe", mm, reads=[("Gs", cg) for cg in range(64)] + ["Wcs"], writes=[f"ps3_{i}"])
        cb = (t1 // 8) % 2
        dst = Ych[cb][:, t1 % 8, :]
        if t1 % 2 == 0:
            P.op("act", lambda e, dst=dst, i=i: e.activation(out=dst, in_=ps3[i][:, 0:256], func=AF.Copy), reads=[f"ps3_{i}"], writes=[f"Ych{cb}"])
        else:
            P.op("dve", lambda e, dst=dst, i=i: e.tensor_copy(out=dst, in_=ps3[i][:, 0:256]), reads=[f"ps3_{i}"], writes=[f"Ych{cb}"])
        if t1 % 8 == 7:
            t0 = t1 - 7
            P.dma("sp", f"sty{cb}", lambda e, cb=cb, t0=t0: e.dma_start(out=ydv[:, t0:t0 + 8, :], in_=Ych[cb][:]), reads=[f"Ych{cb}"])
    return C.done()


def dft_consts():
    s1 = np.arange(64)[:, None].astype(np.float64)
    t1 = np.arange(64)[None, :].astype(np.float64)
    a = 2 * np.pi * s1 * t1 / 64
    c1, sn1 = np.cos(a), np.sin(a)
    w1 = np.zeros((128, 128))
    w1[0:64, 0:64] = c1
    w1[0:64, 64:128] = -sn1
    w1[64:128, 0:64] = -sn1
    w1[64:128, 64:128] = -c1
    s2 = np.arange(128)[:, None, None].astype(np.float64)
    tt1 = np.arange(64)[None, :, None].astype(np.float64)
    t2 = np.arange(128)[None, None, :].astype(np.float64)
    ang = 2 * np.pi * ((64 * t2 + tt1) * s2 % 8192) / 8192
    wcs = np.zeros((128, 64, 2, 128))
    wcs[:, :, 0, :] = np.cos(ang) / np.sqrt(8192.0)
    wcs[:, :, 1, :] = np.sin(ang) / np.sqrt(8192.0)
    m = np.arange(512)[:, None].astype(np.float64)
    mm = np.arange(512)[None, :].astype(np.float64)
    ac = 2 * np.pi * (m * mm % 512) / 512
    cs = np.concatenate([np.cos(ac), np.sin(ac)], axis=1) / np.sqrt(512.0)
    return _bf(w1.astype(np.float32)), _bf(wcs.reshape(128, -1).astype(np.float32)), cs.astype(np.float32)


def _run(nc, ims):
    return run_bass_kernel_spmd(nc, ims, core_ids=list(range(NCORES))).results


def run_tokenC(x, o_n, mods, norm_gain, w_in, w_out, sg_w, sg_b, sg_ln_gain, sg_ln_bias, w1, w2, cs):
    nc = build_token("C")
    mx0, mx1 = mods[0, 0], mods[0, 1]
    ch = lambda v, i: v[i * D:(i + 1) * D]
    rows = np.stack([norm_gain[0, 0], ch(mx0, 0), ch(mx0, 1), ch(mx0, 2), norm_gain[0, 1], norm_gain[0, 2], ch(mx0, 3), ch(mx0, 4),
                     ch(mx0, 5), norm_gain[0, 3], norm_gain[1, 0], ch(mx1, 0), ch(mx1, 1)], 0).astype(np.float32)
    wguv = np.ascontiguousarray(w_in[0][:, 4096:7168])
    sgwT = np.ascontiguousarray(sg_w[0].transpose(0, 2, 1))
    sgb = np.ascontiguousarray(sg_b[0].reshape(1, 1024))
    lnd = np.stack([sg_ln_gain[0], sg_ln_bias[0]], 0).astype(np.float32)
    ident = np.eye(128, dtype=np.float32)
    ims = []
    for j in range(NCORES):
        sl = slice(j * 1024, (j + 1) * 1024)
        ims.append({"xres": np.ascontiguousarray(x[0, sl]), "rows": rows, "wmix": w_out[0], "w1": w1, "w2": w2,
                    "oT": np.ascontiguousarray(o_n[sl].T), "wguv": wguv, "sgwT": sgwT, "sgb": sgb, "lnd": lnd, "csd": cs, "identd": ident})
    res = _run(nc, ims)
    x2 = np.concatenate([res[j]["x2"] for j in range(NCORES)], axis=0)
    ab = np.concatenate([res[j]["AB"] for j in range(NCORES)], axis=0)
    return x2, ab


def run_seqdft(ab, w1b, wcsb):
    nc = build_seqdft()
    ims = []
    for j in range(NCORES):
        g, m0 = j // 2, (j % 2) * 256
        ims.append({"ain": np.ascontiguousarray(ab[:, g * 1024 + m0:g * 1024 + m0 + 256]),
                    "bin": np.ascontiguousarray(ab[:, g * 1024 + 512 + m0:g * 1024 + 512 + m0 + 256]), "w1d": w1b, "wcsd": wcsb})
    res = _run(nc, ims)
    return np.concatenate([res[j]["y"] for j in range(NCORES)], axis=1)


def run_tokenE(x2, y, mods, norm_gain, w_f, w1, w2):
    nc = build_token("E")
    mx1 = mods[0, 1]
    ch = lambda v, i: v[i * D:(i + 1) * D]
    rows = np.stack([ch(mx1, 2), norm_gain[1, 1], norm_gain[1, 2], ch(mx1, 3), ch(mx1, 4), ch(mx1, 5), norm_gain[1, 3]], 0).astype(np.float32)
    ident = np.eye(128, dtype=np.float32)
    ims = []
    for j in range(NCORES):
        sl = slice(j * 1024, (j + 1) * 1024)
        ims.append({"xres": np.ascontiguousarray(x2[sl]), "rows": rows, "wmix": w_f, "w1": w1, "w2": w2,
                    "YT": np.ascontiguousarray(y[sl].T), "identd": ident})
    res = _run(nc, ims)
    return np.concatenate([res[j]["xo"] for j in range(NCORES)], axis=0)


def kernel(x, c, ctx, c_ctx, w_ada, b_ada, norm_gain, w_in, w_out, lb_raw, hg_norm_gain,
           sg_w, sg_b, sg_ln_gain, sg_ln_bias, w_fourier, w_mlp_in, w_mlp_out):
    f = lambda a: np.asarray(a, dtype=np.float32)
    x, c, ctx, c_ctx, w_ada, b_ada, norm_gain, w_in, w_out, lb_raw, hg_norm_gain = map(
        f, (x, c, ctx, c_ctx, w_ada, b_ada, norm_gain, w_in, w_out, lb_raw, hg_norm_gain))
    sg_w, sg_b, sg_ln_gain, sg_ln_bias, w_fourier, w_mlp_in, w_mlp_out = map(
        f, (sg_w, sg_b, sg_ln_gain, sg_ln_bias, w_fourier, w_mlp_in, w_mlp_out))
    w1b, wcsb, cs = dft_consts()
    mods = run_mods(c, c_ctx, w_ada, b_ada)
    o_n = run_hgrn(x, ctx, mods, norm_gain, w_in, lb_raw, hg_norm_gain)
    x2, ab = run_tokenC(x, o_n, mods, norm_gain, w_in, w_out, sg_w, sg_b, sg_ln_gain, sg_ln_bias, w_mlp_in[0], w_mlp_out[0], cs)
    y = run_seqdft(ab, w1b, wcsb)
    out = run_tokenE(x2, y, mods, norm_gain, w_fourier[0], w_mlp_in[1], w_mlp_out[1])
    return out.reshape(1, SEQ, D).astype(np.float32)
```
